# Optimizing a Trainium2 kernel written in Bass

```python
import jax, jax.numpy as jnp
from jax import lax
import numpy as np

D_MODEL = 1024
BATCH = 4
SEQ = 8192
DEPTH = 1

CHUNK = 64
N_LEFT_CHUNKS = 8
BAND = (N_LEFT_CHUNKS + 1) * CHUNK
MIX_WIDTH = D_MODEL
ATTN_WIDTH = MIX_WIDTH // 2
HEAD_DIM = 64
N_HEADS = ATTN_WIDTH // HEAD_DIM
REL_CLIP = 256
POOL_WIDTH = MIX_WIDTH - ATTN_WIDTH
POOL_WINDOWS = (2, 4, 8, 16)
N_POOL_GROUPS = len(POOL_WINDOWS)
POOL_GROUP = POOL_WIDTH // N_POOL_GROUPS
IN_WIDTH = 3 * ATTN_WIDTH + POOL_WIDTH + MIX_WIDTH
EPS = 1e-6
NEG_INF = -1e30

kernel_name = "hybrid_chunk_attn_pool_block"


def rms_norm(x, g):
    xf = x.astype(jnp.float32)
    y = xf * lax.rsqrt(jnp.mean(xf * xf, axis=-1, keepdims=True) + EPS)
    return (y * g.astype(jnp.float32)).astype(x.dtype)


def chunk_attention(q, k, v, rel_bias):
    B, S, H, Dh = q.shape
    n_chunks = S // CHUNK
    pad = N_LEFT_CHUNKS * CHUNK
    kp = jnp.pad(k, ((0, 0), (pad, 0), (0, 0), (0, 0)))
    vp = jnp.pad(v, ((0, 0), (pad, 0), (0, 0), (0, 0)))
    a = jnp.arange(CHUNK)[:, None]
    j = jnp.arange(BAND)[None, :]
    rel = j - pad - a
    idx = jnp.clip(rel, -REL_CLIP, REL_CLIP) + REL_CLIP
    bias = rel_bias.astype(jnp.float32)[:, idx]
    scale = HEAD_DIM ** -0.5
    band_offsets = jnp.arange(BAND) - pad

    def one_chunk(n):
        start = n * CHUNK
        qn = lax.dynamic_slice_in_dim(q, start, CHUNK, axis=1)
        kn = lax.dynamic_slice_in_dim(kp, start, BAND, axis=1)
        vn = lax.dynamic_slice_in_dim(vp, start, BAND, axis=1)
        s = jnp.einsum('bqhd,bkhd->bhqk', qn, kn).astype(jnp.float32) * scale + bias[None]
        valid = (start + band_offsets) >= 0
        s = jnp.where(valid[None, None, None, :], s, NEG_INF)
        p = jax.nn.softmax(s, axis=-1).astype(vn.dtype)
        return jnp.einsum('bhqk,bkhd->bqhd', p, vn)

    out = lax.map(one_chunk, jnp.arange(n_chunks))
    return jnp.transpose(out, (1, 0, 2, 3, 4)).reshape(B, S, H * Dh)


def multiscale_pool(u, w_pool, pool_scale):
    B, S, _ = u.shape
    uf = u.astype(jnp.float32).reshape(B, S, N_POOL_GROUPS, POOL_GROUP)
    cs = jnp.concatenate([jnp.zeros((B, 1, N_POOL_GROUPS, POOL_GROUP), jnp.float32),
                          jnp.cumsum(uf, axis=1)], axis=1)
    t = jnp.arange(S)[:, None]
    win = jnp.array(POOL_WINDOWS, dtype=jnp.int32)[None, :]
    lo = jnp.maximum(t + 1 - win, 0)
    cnt = jnp.minimum(t + 1, win).astype(jnp.float32)
    g_idx = jnp.arange(N_POOL_GROUPS)[None, :]
    window_sum = cs[:, 1:] - cs[:, lo, g_idx]
    mixed = window_sum / cnt[None, :, :, None] - uf
    y = jnp.einsum('bsgi,gio->bsgo', mixed.astype(u.dtype), w_pool)
    return y.reshape(B, S, POOL_WIDTH) * pool_scale


def setup_inputs(seed: int = 0) -> dict:
    key = jax.random.key(seed)
    ks = jax.random.split(key, 12)
    f32 = jnp.float32
    x = jax.random.normal(ks[0], (BATCH, SEQ, D_MODEL), f32)
    c = jax.random.normal(ks[1], (BATCH, D_MODEL), f32)
    norm_g = 1.0 + 0.02 * jax.random.normal(ks[2], (DEPTH, D_MODEL), f32)
    w_ada = 0.5 * D_MODEL ** -0.5 * jax.random.normal(ks[3], (DEPTH, D_MODEL, 3 * D_MODEL), f32)
    b_ada = 0.02 * jax.random.normal(ks[4], (DEPTH, 3 * D_MODEL), f32)
    w_in = D_MODEL ** -0.5 * jax.random.normal(ks[5], (DEPTH, D_MODEL, IN_WIDTH), f32)
    q_norm_g = 1.0 + 0.02 * jax.random.normal(ks[6], (DEPTH, HEAD_DIM), f32)
    k_norm_g = 1.0 + 0.02 * jax.random.normal(ks[7], (DEPTH, HEAD_DIM), f32)
    rel_bias = 0.5 * jax.random.normal(ks[8], (DEPTH, N_HEADS, 2 * REL_CLIP + 1), f32)
    w_pool = POOL_GROUP ** -0.5 * jax.random.normal(ks[9], (DEPTH, N_POOL_GROUPS, POOL_GROUP, POOL_GROUP), f32)
    pool_scale = 1.0 + 0.02 * jax.random.normal(ks[10], (DEPTH, POOL_WIDTH), f32)
    w_out = MIX_WIDTH ** -0.5 * jax.random.normal(ks[11], (DEPTH, MIX_WIDTH, D_MODEL), f32)
    return {"x": x, "c": c, "norm_g": norm_g, "w_ada": w_ada, "b_ada": b_ada,
            "w_in": w_in, "q_norm_g": q_norm_g, "k_norm_g": k_norm_g,
            "rel_bias": rel_bias, "w_pool": w_pool, "pool_scale": pool_scale,
            "w_out": w_out}


def reference(x, c, norm_g, w_ada, b_ada, w_in, q_norm_g, k_norm_g, rel_bias,
              w_pool, pool_scale, w_out):
    B, S, _ = x.shape
    for l in range(DEPTH):
        mod = c @ w_ada[l] + b_ada[l]
        shift, scale, gate = jnp.split(mod, 3, axis=-1)
        h = rms_norm(x, norm_g[l]) * (1.0 + scale[:, None, :]) + shift[:, None, :]
        proj = h @ w_in[l]
        o1 = ATTN_WIDTH
        o2 = 2 * ATTN_WIDTH
        o3 = 3 * ATTN_WIDTH
        o4 = o3 + POOL_WIDTH
        q = rms_norm(proj[..., :o1].reshape(B, S, N_HEADS, HEAD_DIM), q_norm_g[l])
        k = rms_norm(proj[..., o1:o2].reshape(B, S, N_HEADS, HEAD_DIM), k_norm_g[l])
        v = proj[..., o2:o3].reshape(B, S, N_HEADS, HEAD_DIM)
        u = proj[..., o3:o4]
        z = proj[..., o4:]
        a_out = chunk_attention(q, k, v, rel_bias[l])
        p_out = multiscale_pool(u, w_pool[l], pool_scale[l])
        y = jnp.concatenate([a_out, p_out], axis=-1) * jax.nn.silu(z)
        x = x + gate[:, None, :] * (y @ w_out[l])
    return x
```

```python
import numpy as np
import ml_dtypes
from contextlib import ExitStack
import concourse.bass as bass
import concourse.mybir as mybir
from concourse.bass_utils import run_bass_kernel_spmd

F32 = mybir.dt.float32
BF16 = mybir.dt.bfloat16
AF = mybir.ActivationFunctionType
ALU = mybir.AluOpType
AX = mybir.AxisListType

D = 1024
NCORES = 8
TOWN = 4096
NHALO = 4
NOWN = 32
NT = NHALO + NOWN
NSLOT = 8
XR = 3
XRR = 3
NHT = 2
NQT = 3
NUB = 3
NYB = 2
NSZ = 3
NSB = 3
NPT = 4
NSTG = 8
EPS = 1e-6
NEG = -30000.0
SCHED_LAT = 0.45
SCHED_SEED = 0
SCHED_AMP = 0.0


class _Op:
    __slots__ = ("eng", "fn", "reads", "writes", "dsem", "deps", "signal", "sig", "odeps", "busy", "lat", "idx", "t_end", "nb", "gb", "gl")

    def __init__(self, eng, fn, reads, writes, dsem):
        self.odeps = []
        self.busy = 0.0
        self.lat = 0.0
        self.idx = 0
        self.t_end = None
        self.nb = 0
        self.eng = eng
        self.fn = fn
        self.reads = reads
        self.writes = writes
        self.dsem = dsem
        self.deps = []
        self.signal = dsem is not None
        self.sig = None


class Prog:
    ENGS = ("pe", "act", "dve", "pool", "sp")

    def __init__(self, nc):
        self.nc = nc
        self.ops = []
        self.last_writer = {}
        self.readers = {}
        self.dsem_names = []
        self.setup_keys = {}
        self.alias_keys = {}

    @staticmethod
    def _cost(eng, dsem, n):
        if dsem is not None:
            nb = 524288 if n is None else n
            return (0.45 if eng != "pool" else 1.0), 2.0 + nb / 240e3
        if n is None:
            n = 512
        if eng == "pe":
            b = 0.012 + max(n, 48) / 2300.0
        elif eng == "act":
            b = 0.2 + n / 1200.0
        elif eng == "dve":
            b = 0.07 + n / 960.0
        else:
            b = 0.3 + n / 300.0
        return b, b + SCHED_LAT

    def op(self, eng, fn, reads=(), writes=(), dsem=None, n=None):
        reads = list(reads)
        writes = list(writes)
        for g in self.setup_keys:
            if any(k in self.setup_keys[g] for k in reads + writes):
                reads.append("SS_" + g)
            if any(k in self.alias_keys[g] for k in writes):
                reads.append("SSF_" + g)
        o = _Op(eng, fn, reads, writes, dsem)
        o.busy, o.lat = self._cost(eng, dsem, n)
        if dsem is not None:
            o.nb = 524288 if n is None else n
        o.idx = len(self.ops)
        deps = set()
        for b in reads:
            w = self.last_writer.get(b)
            if w is not None:
                deps.add(w)
        for b in writes:
            w = self.last_writer.get(b)
            if w is not None:
                deps.add(w)
            deps.update(self.readers.get(b, ()))
        deps.discard(o)
        for d in deps:
            if d.eng == "pe" and eng == "pe" and d.dsem is None and dsem is None:
                o.odeps.append(d)
                continue
            o.deps.append(d)
        for b in reads:
            self.readers.setdefault(b, []).append(o)
        for b in writes:
            self.last_writer[b] = o
            self.readers[b] = []
        if dsem is not None and dsem not in self.dsem_names:
            self.dsem_names.append(dsem)
        self.ops.append(o)
        return o

    def schedule(self, window=100000):
        pend = {e: [] for e in self.ENGS}
        for o in self.ops:
            pend[o.eng].append(o)
        head = {e: 0 for e in self.ENGS}
        free = {e: 0.0 for e in self.ENGS}
        order = {e: [] for e in self.ENGS}
        remaining = len(self.ops)
        dma_free = 0.0
        rs = np.random.RandomState(SCHED_SEED)
        for o in self.ops:
            f = 1.0 + SCHED_AMP * (2.0 * rs.rand() - 1.0) if SCHED_AMP > 0 else 1.0
            o.gb = o.busy * f
            o.gl = o.lat * f
        while remaining:
            best = None
            for e in self.ENGS:
                L = pend[e]
                h = head[e]
                while h < len(L) and L[h].t_end is not None:
                    h += 1
                head[e] = h
                seen = 0
                k = h
                while k < len(L) and seen < window:
                    o = L[k]
                    k += 1
                    if o.t_end is not None:
                        continue
                    seen += 1
                    ok = True
                    rdy = free[e]
                    for d in o.deps:
                        if d.t_end is None:
                            ok = False
                            break
                        if d.t_end > rdy:
                            rdy = d.t_end
                    if not ok:
                        continue
                    for d in o.odeps:
                        if d.t_end is None:
                            ok = False
                            break
                    if not ok:
                        continue
                    key = (rdy, o.idx)
                    if best is None or key < best[0]:
                        best = (key, o)
                    if rdy <= free[e]:
                        break
            key, o = best
            st_ = key[0]
            free[o.eng] = st_ + o.gb
            if o.dsem is not None:
                xs = max(st_, dma_free)
                dma_free = xs + o.nb / 240e3
                o.gl = dma_free + 2.0 - st_
            o.t_end = st_ + o.gl
            order[o.eng].append(o)
            remaining -= 1
        self.sim_time = max(free.values())
        self.ops = sorted(self.ops, key=lambda o: (o.t_end - o.gl, o.idx))
        return order

    def emit(self, final_wait_eng="pool", reorder=True):
        nc = self.nc
        if reorder:
            self.schedule()
        pos = {}
        for k, o in enumerate(self.ops):
            pos[o] = k
        for o in self.ops:
            keep = []
            last = {}
            for d in o.deps:
                if d.dsem is not None:
                    keep.append(d)
                elif d.eng not in last or pos[d] > pos[last[d.eng]]:
                    last[d.eng] = d
            keep.extend(last.values())
            o.deps = keep
            for d in keep:
                d.signal = True
        cnt = {e: 0 for e in self.ENGS}
        dcnt = {d: 0 for d in self.dsem_names}
        for o in self.ops:
            if o.dsem is not None:
                dcnt[o.dsem] += 16
                o.sig = ("d", o.dsem, dcnt[o.dsem])
            elif o.signal:
                cnt[o.eng] += 1
                o.sig = ("e", o.eng, cnt[o.eng])
        for o in self.ops:
            if o.dsem is not None and o.dsem.startswith("G:"):
                o.sig = ("d", o.dsem, dcnt[o.dsem])
        with ExitStack() as st:
            esem = {e: st.enter_context(nc.semaphore("s_" + e)) for e in self.ENGS}
            dsem = {d: st.enter_context(nc.semaphore("d_%d" % i)) for i, d in enumerate(self.dsem_names)}
            block = st.enter_context(nc.Block())

            def run(engname, eng):
                waited = {}
                for o in self.ops:
                    if o.eng != engname:
                        continue
                    need = {}
                    for d in o.deps:
                        k = d.sig[:2]
                        if d.sig[2] > need.get(k, 0):
                            need[k] = d.sig[2]
                    for k, v in need.items():
                        if waited.get(k, 0) >= v:
                            continue
                        waited[k] = v
                        eng.wait_ge(esem[k[1]] if k[0] == "e" else dsem[k[1]], v)
                    ins = o.fn(eng)
                    if o.dsem is not None:
                        ins.then_inc(dsem[o.dsem], 16)
                    elif o.signal:
                        ins.then_inc(esem[o.eng], 1)
                if engname == final_wait_eng:
                    for d in self.dsem_names:
                        if dcnt[d] > waited.get(("d", d), 0):
                            eng.wait_ge(dsem[d], dcnt[d])

            @block.tensor
            def _(e):
                run("pe", e)

            @block.scalar
            def _(e):
                run("act", e)

            @block.vector
            def _(e):
                run("dve", e)

            @block.gpsimd
            def _(e):
                run("pool", e)

            @block.sync
            def _(e):
                run("sp", e)


def build_program():
    nc = bass.Bass("TRN2", target_bir_lowering=False)

    def din(name, shape, dt=F32):
        return nc.dram_tensor(name, shape, dt, kind="ExternalInput").ap()

    xo = din("xo", [TOWN, D])
    xh = din("xh", [NHALO * 128, D])
    c8_d = din("c8", [128, 8])
    hv_d = din("hv", [128, 1])
    w_ada = din("w_ada", [D, 3 * D])
    b_ada = din("b_ada", [1, 3 * D])
    norm_g = din("norm_g", [1, D])
    w_in = din("w_in", [D, 3 * D])
    w_out = din("w_out", [D, D])
    gq2_d = din("gq2", [128, 1])
    gk2_d = din("gk2", [128, 1])
    biasT_d = din("biasT", [128, 5 * 2 * 512])
    wpl_d = din("wpl", [128, 512])
    pscale = din("pscale", [1, 512])
    Ac_d = din("AcT", [128, 512], BF16)
    Ap_d = din("ApT", [128, 512], BF16)
    Ac0_d = din("Ac0T", [128, 512], BF16)
    Ap0_d = din("Ap0T", [128, 512], BF16)
    ident_d = din("ident", [128, 128], BF16)
    icnt0_d = din("icnt0", [128, 4])
    out = nc.dram_tensor("out", [TOWN, D], F32, kind="ExternalOutput").ap()

    with ExitStack() as st:
        def sb(name, shape, dt):
            return st.enter_context(nc.sbuf_tensor("sb_" + name, shape, dt))

        Wb = sb("Wb", [128, 8, 3 * D], BF16)
        Wob = sb("Wob", [128, 8, D], BF16)
        wpb = sb("wpb", [128, 4, 128], BF16)
        biasT = sb("biasT", [128, 5, 2, 512], F32)
        Ac = sb("Ac", [128, 512], BF16)
        Ap = sb("Ap", [128, 512], BF16)
        Ac0 = sb("Ac0", [128, 512], BF16)
        Ap0 = sb("Ap0", [128, 512], BF16)
        ident = sb("ident", [128, 128], BF16)
        mod = sb("mod", [128, 3 * D], F32)
        c8 = sb("c8", [128, 8], F32)
        hv = sb("hv", [128, 1], F32)
        gq2 = sb("gq2", [128, 1], F32)
        gk2 = sb("gk2", [128, 1], F32)
        G = sb("G", [128, 1], F32)
        icnt0 = sb("icnt0", [128, 4], F32)
        ones = sb("ones", [128, 128], F32)
        fence = sb("fence", [128, 2], F32)
        kT = sb("kT", [128, NSLOT, 4, 128], BF16)
        Vaug = sb("Vaug", [128, NSLOT, 8, 66], BF16)
        xt = [sb("xt%d" % i, [128, D], F32) for i in range(XR)]
        hb = [sb("hb%d" % i, [128, D], BF16) for i in range(2)]
        hT = [sb("hT%d" % i, [128, 8, 128], BF16) for i in range(NHT)]
        sqf = [sb("sqf%d" % i, [128, 512], F32) for i in range(2)]
        qkn = [sb("qkn%d" % i, [128, 512], BF16) for i in range(4)]
        qT = [sb("qT%d" % i, [128, 4, 128], BF16) for i in range(NQT)]
        ub = [sb("ub%d" % i, [128, 512], BF16) for i in range(NUB)]
        ybt = [sb("yb%d" % i, [128, D], BF16) for i in range(NYB)]
        yTt = [sb("yT%d" % i, [128, D], BF16) for i in range(2)]
        ss = sb("ss", [128, 2], F32)
        lnv = sb("lnv", [128, 2], F32)
        rstd = sb("rstd", [128, 2], F32)
        ssq = [sb("ssq%d" % i, [128, 8], F32) for i in range(2)]
        lnq = [sb("lnq%d" % i, [128, 8], F32) for i in range(2)]
        rq = [sb("rq%d" % i, [128, 8], F32) for i in range(2)]
        rden = [sb("rden%d" % i, [128, 2, 4], F32) for i in range(2)]
        shE = sb("shE", [128, 20 * 256], F32)
        shL = sb("shL", [128, NSTG * 512], F32)

        def she(a_kb, b_kb):
            return shE[:, int(a_kb * 256):int(b_kb * 256)]

        acc = she(0, 12)
        ng_bc = she(12, 16)
        ps_bc = she(16, 18)
        wst = she(18, 20)
        sz = [she(4 * i, 4 * i + 4) for i in range(NSZ)]
        ez = [she(12, 14), she(14, 16)]
        mixT = [she(16, 17).bitcast(BF16), she(17, 18).bitcast(BF16)]
        on = [she(18, 19), she(19, 20)]
        stage = [shL[:, i * 512:(i + 1) * 512] for i in range(NSTG)]
        xr = [shL[:, i * 1024:(i + 1) * 1024] for i in range(XRR)]
        sbbt = [sb("sbb%d" % i, [128, 512], F32) for i in range(NSB)]
        pTt = [sb("pT%d" % i, [128, 512], BF16) for i in range(NPT)]
        sbb = [t[:] for t in sbbt]
        pT = [t[:] for t in pTt]
        yb = [t[:] for t in ybt]
        yT = [t[:] for t in yTt]

        banks = [st.enter_context(nc.psum_tensor("bank%d" % i, [128, 512], F32)) for i in range(8)]
        banks_bf = [b.bitcast(BF16) for b in banks]

        P = Prog(nc)
        P.setup_keys = {"E": set(["acc", "ng_bc", "ps_bc", "wst"] + [("acc", n) for n in range(4)]),
                        "L": set("stage%d" % i for i in range(NSTG))}
        P.alias_keys = {"E": set([("sz", i) for i in range(4)] + [("ez", i) for i in range(2)] + [("mixT", i) for i in range(2)] + [("on", i) for i in range(2)]),
                        "L": set(("xr", i) for i in range(XRR))}

        def ld(dst, src, key, sem="G:c"):
            P.op("act", lambda e: e.dma_start(out=dst, in_=src), writes=[key], dsem=sem, n=65536)

        ld(c8[:], c8_d, "c8")
        ld(mod[:, 0:2 * D], b_ada[:, 0:2 * D].partition_broadcast(128), "modb")
        ld(ng_bc, norm_g.partition_broadcast(128), "ng_bc")
        ld(ident[:], ident_d, "ident")
        ld(hv[:], hv_d, "hv")
        ld(icnt0[:], icnt0_d, "icnt0")
        ld(gq2[:], gq2_d, "gq2")
        ld(gk2[:], gk2_d, "gk2")
        ld(Ac[:], Ac_d, "Ac")
        ld(Ap[:], Ap_d, "Ap")
        ld(Ac0[:], Ac0_d, "Ac0")
        ld(Ap0[:], Ap0_d, "Ap0")
        ld(wst, wpl_d, "wst")
        ld(ps_bc, pscale.partition_broadcast(128), "ps_bc")

        P.op("dve", lambda e: e.memset(ones[:], 1.0), writes=["ones"], n=128)
        P.op("pool", lambda e: e.memset(Vaug[:].rearrange("p a b c -> p (a b c)"), 1.0),
             writes=[("V", s) for s in range(NSLOT)] + [("V1", s) for s in range(NSLOT)], n=1000)
        P.op("dve", lambda e: e.scalar_tensor_tensor(out=G[:], in0=gq2[:], scalar=0.125, in1=gk2[:], op0=ALU.mult, op1=ALU.mult),
             reads=["gq2", "gk2"], writes=["G"], n=1)
        P.op("dve", lambda e: e.tensor_tensor(out=wpb[:].rearrange("p g o -> p (g o)"), in0=wst, in1=ps_bc, op=ALU.mult),
             reads=["wst", "ps_bc"], writes=["wpb"])

        stg_i = [0]

        def staged(src, consumer):
            s = stg_i[0] % NSTG
            stg_i[0] += 1
            P.op("sp", lambda e: e.dma_start(out=stage[s], in_=src), writes=["stage%d" % s], dsem="stg%d" % s, n=262144)
            consumer(stage[s], "stage%d" % s)

        arot = [0]
        brot = [0]

        def bankA():
            i = 2 + arot[0] % 3
            arot[0] += 1
            return i

        def bankB():
            i = 5 + brot[0] % 3
            brot[0] += 1
            return i

        def load_x(j):
            src = xh[j * 128:(j + 1) * 128, :] if j < NHALO else xo[(j - NHALO) * 128:(j - NHALO + 1) * 128, :]
            P.op("sp", lambda e: e.dma_start(out=xt[j % XR][:], in_=src), writes=[("xt", j % XR)], dsem="ld%d" % (j % XR))

        def ada_block(n, fin=True):
            cs = slice(n * 512, (n + 1) * 512)
            late = n >= 4
            accn = mod[:, cs] if late else acc[:, cs]
            kacc = ("accg", n) if late else ("acc", n)
            for kc in range(8):
                def cons(stg, key, kc=kc):
                    if kc == 0:
                        P.op("dve", lambda e: e.tensor_scalar(out=accn, in0=stg, scalar1=c8[:, 0:1], scalar2=None, op0=ALU.mult),
                             reads=[key, "c8"], writes=[kacc])
                    else:
                        P.op("dve", lambda e: e.scalar_tensor_tensor(out=accn, in0=stg, scalar=c8[:, kc:kc + 1],
                                                                     in1=accn, op0=ALU.mult, op1=ALU.add),
                             reads=[key, "c8", kacc], writes=[kacc])
                staged(w_ada[kc * 128:(kc + 1) * 128, cs], cons)
            def fin_part():
                bi = bankA() if not late else bankB()
                P.op("pe", lambda e: e.matmul(banks[bi][:, :], lhsT=ones[:], rhs=accn, start=True, stop=True),
                     reads=["ones", kacc], writes=[("bank", bi)], n=2048)
                if late:
                    def cons2(stg, key):
                        P.op("dve", lambda e: e.tensor_tensor(out=mod[:, cs], in0=banks[bi][:, :], in1=stg, op=ALU.add),
                             reads=[("bank", bi), key, kacc], writes=[("mod", n), kacc])
                    staged(b_ada[:, cs].partition_broadcast(128), cons2)
                else:
                    P.op("dve", lambda e: e.tensor_tensor(out=mod[:, cs], in0=banks[bi][:, :], in1=mod[:, cs], op=ALU.add),
                         reads=[("bank", bi), "modb"], writes=[("mod", n)])
            if fin:
                fin_part()
                return None
            return fin_part

        for n in range(4):
            ada_block(n)
        P.op("dve", lambda e: e.scalar_tensor_tensor(out=mod[:, D:2 * D], in0=mod[:, D:2 * D], scalar=1.0, in1=ng_bc, op0=ALU.add, op1=ALU.mult),
             reads=[("mod", 2), ("mod", 3), "ng_bc"], writes=[("mod", 2), ("mod", 3)], n=1024)
        P.op("dve", lambda e: e.memset(fence[:, 0:1], 0.0), writes=["SS_E", "SSF_E"], n=1)
        shift_row = mod[:, 0:D]
        gs_row = mod[:, D:2 * D]
        gate_row = mod[:, 2 * D:3 * D]
        load_x(0)
        load_x(1)
        ci = [0]
        for n in (1, 2, 3, 0, 4, 5):
            cs = slice(n * 512, (n + 1) * 512)
            for kc in range(8):
                def cons(stg, key, kc=kc, n=n, cs=cs):
                    ci[0] += 1
                    if ci[0] % 2 == 0:
                        P.op("dve", lambda e: e.tensor_copy(out=Wb[:, kc, cs], in_=stg), reads=[key], writes=[("Wb", kc, n)])
                    else:
                        P.op("act", lambda e: e.activation(out=Wb[:, kc, cs], in_=stg, func=AF.Copy), reads=[key], writes=[("Wb", kc, n)])
                staged(w_in[kc * 128:(kc + 1) * 128, cs], cons)
        for g in range(4):
            c0 = 1536 + g * 128
            scr = yT[g % 2]
            kscr = ("yT", g % 2)
            bt = bankB()
            for kc in range(8):
                P.op("pe", lambda e, kc=kc, bt=bt, c0=c0: e.transpose(out=banks_bf[bt][:, kc * 128:(kc + 1) * 128], in_=Wb[:, kc, c0:c0 + 128], identity=ident[:]),
                     reads=[("Wb", kc, 3), "ident"], writes=[("bank", bt)], n=128)
            P.op("dve", lambda e, bt=bt, scr=scr: e.tensor_copy(out=scr, in_=banks_bf[bt][:, :]), reads=[("bank", bt)], writes=[kscr], n=600)
            for half in range(2):
                bm = bankB()
                for q in range(4):
                    kc = half * 4 + q
                    P.op("pe", lambda e, kc=kc, q=q, bm=bm, scr=scr, g=g: e.matmul(banks[bm][:, q * 128:(q + 1) * 128], lhsT=scr[:, kc * 128:(kc + 1) * 128],
                                                                                  rhs=wpb[:, g, :], start=True, stop=True),
                         reads=[kscr, "wpb"], writes=[("bank", bm)], n=128)
                kws = [("Wb", kc, 3) for kc in range(half * 4, half * 4 + 4)]
                P.op("act", lambda e, half=half, bm=bm, c0=c0: e.activation(out=Wb[:, half * 4:half * 4 + 4, c0:c0 + 128],
                                                                            in_=banks[bm][:, :].rearrange("p (a b) -> p a b", b=128), func=AF.Copy),
                     reads=[("bank", bm)] + kws, writes=kws)
        P.op("sp", lambda e: e.dma_start(out=biasT[:].rearrange("p a b c -> p (a b c)"), in_=biasT_d), reads=[("Wb", 7, 5)], writes=["biasT"], dsem="bias", n=2621440)
        late_fins = [ada_block(n, fin=False) for n in (4, 5)]

        def late_setup():
            for f in late_fins:
                f()
            bankB()
            for n in range(2):
                cs = slice(n * 512, (n + 1) * 512)
                for kc in range(8):
                    def cons(stg, key, kc=kc, n=n, cs=cs):
                        if kc % 2 == 0:
                            P.op("pool", lambda e: e.tensor_tensor(out=Wob[:, kc, cs], in0=stg, in1=gate_row[:, cs], op=ALU.mult),
                                 reads=[key, ("mod", 4 + n)], writes=[("Wob", kc, n)], n=330)
                        else:
                            P.op("dve", lambda e: e.tensor_tensor(out=Wob[:, kc, cs], in0=stg, in1=gate_row[:, cs], op=ALU.mult),
                                 reads=[key, ("mod", 4 + n)], writes=[("Wob", kc, n)])
                    staged(w_out[kc * 128:(kc + 1) * 128, cs], cons)
            P.op("dve", lambda e: e.memset(fence[:, 1:2], 0.0), writes=["SS_L", "SSF_L"], n=1)

        def load_xr(j):
            i = j - NHALO
            P.op("sp", lambda e: e.dma_start(out=xr[j % XRR], in_=xo[i * 128:(i + 1) * 128, :]), writes=[("xr", j % XRR)], dsem="lr%d" % (j % XRR))

        def stage1(j, A):
            xtj = xt[j % XR]
            hj = hb[j % 2]
            hTj = hT[j % NHT]
            kx = ("xt", j % XR)
            kh = ("hb", j % 2)
            khT = ("hT", j % NHT)
            A("act", lambda e: e.activation(out=hj[:], in_=xtj[:], func=AF.Square, accum_out=ss[:, 0:1]), reads=[kx], writes=[kh, "ss"], n=1024)
            A("act", lambda e: e.activation(out=lnv[:, 0:1], in_=ss[:, 0:1], func=AF.Ln, scale=1.0 / D, bias=EPS), reads=["ss"], writes=["lnv"], n=1)
            A("act", lambda e: e.activation(out=rstd[:, 0:1], in_=lnv[:, 0:1], func=AF.Exp, scale=-0.5), reads=["lnv"], writes=["rstd"], n=1)
            A("dve", lambda e: e.scalar_tensor_tensor(out=xtj[:], in0=xtj[:], scalar=rstd[:, 0:1], in1=gs_row, op0=ALU.mult, op1=ALU.mult),
              reads=[kx, "rstd", ("mod", 2), ("mod", 3)], writes=[kx], n=1024)
            A(H_ENG, lambda e: e.tensor_tensor(out=hj[:], in0=xtj[:], in1=shift_row, op=ALU.add),
              reads=[kx, ("mod", 0), ("mod", 1)], writes=[kh], n=1024)
            yield
            bi = bankA()
            for kc in range(8):
                A("pe", lambda e, kc=kc, bi=bi: e.transpose(out=banks_bf[bi][:, kc * 128:(kc + 1) * 128], in_=hj[:, kc * 128:(kc + 1) * 128], identity=ident[:]),
                  reads=[kh, "ident"], writes=[("bank", bi)], n=128)
            A("dve", lambda e, bi=bi: e.tensor_copy(out=hTj[:].rearrange("p a b -> p (a b)"), in_=banks_bf[bi][:, :]),
              reads=[("bank", bi)], writes=[khT], n=600)
            yield

        def stage2(j, A):
            halo = j < NHALO
            slot = j % NSLOT
            hTj = hT[j % NHT]
            khT = ("hT", j % NHT)
            chunks = ([2] + ([3] if j == NHALO - 1 else []) + [1]) if halo else [2, 3, 0, 1, 4, 5]
            deferred = []
            for n in chunks:
                bi = bankA()
                kb = ("bank", bi)
                for kc in range(8):
                    A("pe", lambda e, kc=kc, bi=bi, n=n: e.matmul(banks[bi][:, :], lhsT=hTj[:, kc, :], rhs=Wb[:, kc, n * 512:(n + 1) * 512],
                                                                   start=(kc == 0), stop=(kc == 7)),
                      reads=[khT, ("Wb", kc, n)], writes=[kb])
                if n in (0, 1):
                    r = n
                    qn = qkn[(j % 2) * 2 + n]
                    kqn = ("qkn", (j % 2) * 2 + n)
                    A("act", lambda e, bi=bi, r=r: e.activation(out=sqf[r][:], in_=banks[bi][:, :], func=AF.Square), reads=[kb], writes=[("sqf", r)])
                    A("dve", lambda e, r=r: e.tensor_reduce(out=ssq[r][:], in_=sqf[r][:].rearrange("p (h d) -> p h d", d=64), axis=AX.X, op=ALU.add),
                      reads=[("sqf", r)], writes=[("ssq", r)])
                    A("act", lambda e, r=r: e.activation(out=lnq[r][:], in_=ssq[r][:], func=AF.Ln, scale=1.0 / 64, bias=EPS), reads=[("ssq", r)], writes=[("lnq", r)], n=8)
                    A("act", lambda e, r=r: e.activation(out=rq[r][:], in_=lnq[r][:], func=AF.Exp, scale=-0.5), reads=[("lnq", r)], writes=[("rq", r)], n=8)
                    A("dve", lambda e, bi=bi, r=r, qn=qn: e.tensor_tensor(out=qn[:].rearrange("p (h d) -> p h d", d=64),
                                                                         in0=banks[bi][:, :].rearrange("p (h d) -> p h d", d=64),
                                                                         in1=rq[r][:].unsqueeze(2).to_broadcast([128, 8, 64]), op=ALU.mult),
                      reads=[kb, ("rq", r)], writes=[kqn])
                    deferred.append((n, qn, kqn))
                elif n == 2:
                    vdst = Vaug[:, slot, :, 0:64]
                    vsrc = banks[bi][:, :].rearrange("p (h d) -> p h d", d=64)
                    if halo:
                        A("dve", lambda e, vdst=vdst, vsrc=vsrc: e.tensor_scalar(out=vdst, in0=vsrc, scalar1=hv[:, 0:1], scalar2=None, op0=ALU.mult),
                          reads=[kb, "hv"], writes=[("V", slot)])
                        A("dve", lambda e: e.tensor_copy(out=Vaug[:, slot, :, 64:65], in_=hv[:, 0:1].unsqueeze(1).to_broadcast([128, 8, 1])),
                          reads=["hv"], writes=[("V1", slot)], n=8)
                    else:
                        A("act", lambda e, vdst=vdst, vsrc=vsrc: e.activation(out=vdst, in_=vsrc, func=AF.Copy), reads=[kb], writes=[("V", slot)])
                        if NSLOT <= j < NSLOT + NHALO:
                            A("dve", lambda e: e.memset(Vaug[:, slot, :, 64:65], 1.0), writes=[("V1", slot)], n=8)
                elif n == 3:
                    A("act", lambda e, bi=bi: e.activation(out=ub[j % NUB][:], in_=banks[bi][:, :], func=AF.Copy), reads=[kb], writes=[("ub", j % NUB)])
                else:
                    zi = n - 4
                    ezz = ez[zi]
                    kez = ("ez", zi)
                    A("act", lambda e, bi=bi, ezz=ezz: e.activation(out=ezz, in_=banks[bi][:, :], func=AF.Exp, scale=-1.0), reads=[kb], writes=[kez])
                    A("act", lambda e, ezz=ezz: e.activation(out=ezz, in_=ezz, func=AF.Ln, bias=1.0), reads=[kez], writes=[kez])
                    A("act", lambda e, ezz=ezz: e.activation(out=ezz, in_=ezz, func=AF.Exp, scale=-1.0), reads=[kez], writes=[kez])
                    A("dve", lambda e, bi=bi, ezz=ezz, zi=zi: e.tensor_tensor(out=sz[j % NSZ][:, zi * 512:(zi + 1) * 512], in0=banks[bi][:, :], in1=ezz, op=ALU.mult),
                      reads=[kb, kez], writes=[("sz", j % NSZ)])
                yield
            for n, qn, kqn in deferred:
                bt = bankA()
                for hp in range(4):
                    A("pe", lambda e, hp=hp, bt=bt, qn=qn: e.transpose(out=banks_bf[bt][:, hp * 128:(hp + 1) * 128], in_=qn[:, hp * 128:(hp + 1) * 128], identity=ident[:]),
                      reads=[kqn, "ident"], writes=[("bank", bt)], n=128)
                if n == 0:
                    A("dve", lambda e, bt=bt: e.tensor_scalar(out=qT[j % NQT][:].rearrange("p a b -> p (a b)"), in0=banks_bf[bt][:, 0:512],
                                                             scalar1=G[:, 0:1], scalar2=None, op0=ALU.mult),
                      reads=[("bank", bt), "G"], writes=[("qT", j % NQT)])
                else:
                    A("act", lambda e, bt=bt: e.activation(out=kT[:, slot].rearrange("p a b -> p (a b)"), in_=banks_bf[bt][:, 0:512], func=AF.Copy),
                      reads=[("bank", bt)], writes=[("kT", slot)])
                yield

        def stage3(j, B):
            i = j - NHALO
            szj = sz[j % NSZ]
            ksz = ("sz", j % NSZ)
            y = yb[j % NYB]
            ky = ("yb", j % NYB)
            Acur = Ac0 if i == 0 else Ac
            Aprv = Ap0 if i == 0 else Ap
            kAc = "Ac0" if i == 0 else "Ac"
            kAp = "Ap0" if i == 0 else "Ap"
            ucur = ub[j % NUB]
            uprv = ub[(j - 1) % NUB]
            mx = mixT[i % 2]

            def pool_m():
                by = bankB()
                bankB()
                for g in range(4):
                    gs_ = slice(g * 128, (g + 1) * 128)
                    B("pe", lambda e, gs_=gs_, by=by: e.matmul(banks[by][:, gs_], lhsT=Acur[:, gs_], rhs=ucur[:, gs_], start=True, stop=False),
                      reads=[("ub", j % NUB), kAc], writes=[("bank", by)], n=128)
                    B("pe", lambda e, gs_=gs_, by=by: e.matmul(banks[by][:, gs_], lhsT=Aprv[:, gs_], rhs=uprv[:, gs_], start=False, stop=True),
                      reads=[("ub", (j - 1) % NUB), kAp], writes=[("bank", by)], n=128)
                if i == 0:
                    B("dve", lambda e, by=by: e.tensor_tensor(out=banks[by][:, :].rearrange("p (g c) -> p g c", c=128),
                                                              in0=banks[by][:, :].rearrange("p (g c) -> p g c", c=128),
                                                              in1=icnt0[:].unsqueeze(2).to_broadcast([128, 4, 128]), op=ALU.mult),
                      reads=[("bank", by), "icnt0"], writes=[("bank", by)])
                B("dve", lambda e, by=by: e.tensor_tensor(out=y[:, 512:1024], in0=banks[by][:, :], in1=szj[:, 512:1024], op=ALU.mult),
                  reads=[("bank", by), ksz], writes=[ky])

            qTj = qT[j % NQT]
            pts_t = {}

            def qk(t):
                slot = (j - 4 + t) % NSLOT
                bs = [bankB(), bankB()]
                for hh in range(4):
                    for par in range(2):
                        B("pe", lambda e, hh=hh, par=par, bs=bs, slot=slot: e.matmul(
                            banks[bs[par]][:, hh * 128:(hh + 1) * 128],
                            lhsT=kT[par * 64:(par + 1) * 64, slot, hh, :],
                            rhs=qTj[par * 64:(par + 1) * 64, hh, :], start=True, stop=True),
                          reads=[("kT", slot), ("qT", j % NQT)], writes=[("bank", bs[par])], n=64)
                pts = []
                for par in range(2):
                    r = sb_i[0] % NSB
                    pr = sb_i[0] % NPT
                    sb_i[0] += 1
                    B("dve", lambda e, par=par, r=r, bs=bs, t=t: e.tensor_tensor(out=sbb[r], in0=banks[bs[par]][:, :], in1=biasT[:, t, par, :], op=ALU.add),
                      reads=[("bank", bs[par]), "biasT"], writes=[("sbb", r)])
                    B("act", lambda e, r=r, pr=pr: e.activation(out=pT[pr], in_=sbb[r], func=AF.Exp), reads=[("sbb", r)], writes=[("pT", pr)])
                    pts.append(pr)
                pts_t[t] = pts

            def pv(t):
                slot = (j - 4 + t) % NSLOT
                for par in range(2):
                    pr = pts_t[t][par]
                    for hh in range(4):
                        h = 2 * hh + par
                        B("pe", lambda e, par=par, hh=hh, h=h, pr=pr, slot=slot, t=t: e.matmul(
                            banks[par][:, hh * 128:hh * 128 + 65], lhsT=pT[pr][:, hh * 128:(hh + 1) * 128],
                            rhs=Vaug[:, slot, h, 0:65], start=(t == 0 and hh == 0), stop=(t == 4), skip_group_check=True),
                          reads=[("pT", pr), ("V", slot), ("V1", slot)], writes=[("bank", par)], n=65)

            for f, arg in ((qk, 0), (pool_m, None), (qk, 1), (pv, 0), (qk, 2), (pv, 1), (qk, 3), (pv, 2), (qk, 4), (pv, 3), (pv, 4)):
                if arg is None:
                    f()
                else:
                    f(arg)
                yield
            rd = rden[i % 2]
            for par in range(2):
                ov = banks[par][:, :].rearrange("p (h e) -> p h e", e=128)
                B("dve", lambda e, par=par, ov=ov: e.reciprocal(out=rd[:, par, :].unsqueeze(2), in_=ov[:, :, 64:65]),
                  reads=[("bank", par)], writes=[("rden", i % 2, par)], n=4)
                B("dve", lambda e, par=par, ov=ov: e.tensor_tensor(out=on[par].rearrange("p (h d) -> p h d", d=64), in0=ov[:, :, 0:64],
                                                                 in1=rd[:, par, :].unsqueeze(2).to_broadcast([128, 4, 64]), op=ALU.mult),
                  reads=[("bank", par), ("rden", i % 2, par)], writes=[("on", par)], n=256)
                yv = y[:, 0:512].rearrange("p (hh par d) -> p hh par d", par=2, d=64)[:, :, par, :]
                szv = szj[:, 0:512].rearrange("p (hh par d) -> p hh par d", par=2, d=64)[:, :, par, :]
                B("dve", lambda e, par=par, yv=yv, szv=szv: e.tensor_tensor(out=yv, in0=on[par].rearrange("p (h d) -> p h d", d=64), in1=szv, op=ALU.mult),
                  reads=[("on", par), ksz], writes=[ky], n=256)
            yield

        def stage4(j, B):
            i = j - NHALO
            y = yb[j % NYB]
            ky = ("yb", j % NYB)
            xrj = xr[j % XRR]
            kx = ("xr", j % XRR)
            bt = bankB()
            for kc in range(8):
                B("pe", lambda e, kc=kc, bt=bt: e.transpose(out=banks_bf[bt][:, kc * 128:(kc + 1) * 128], in_=y[:, kc * 128:(kc + 1) * 128], identity=ident[:]),
                  reads=[ky, "ident"], writes=[("bank", bt)], n=128)
            yTi = yT[i % 2]
            B("dve", lambda e, bt=bt: e.tensor_copy(out=yTi, in_=banks_bf[bt][:, :]), reads=[("bank", bt)], writes=[("yT", i % 2)], n=600)
            yield
            for n in range(2):
                br = bankB()
                for kc in range(8):
                    B("pe", lambda e, kc=kc, br=br, n=n: e.matmul(banks[br][:, :], lhsT=yTi[:, kc * 128:(kc + 1) * 128], rhs=Wob[:, kc, n * 512:(n + 1) * 512],
                                                                   start=(kc == 0), stop=(kc == 7)),
                      reads=[("yT", i % 2), ("Wob", kc, n)], writes=[("bank", br)])
                B("dve", lambda e, br=br, n=n: e.tensor_tensor(out=xrj[:, n * 512:(n + 1) * 512], in0=banks[br][:, :], in1=xrj[:, n * 512:(n + 1) * 512], op=ALU.add),
                  reads=[("bank", br), kx], writes=[kx])
                B("sp", lambda e, n=n: e.dma_start(out=out[i * 128:(i + 1) * 128, n * 512:(n + 1) * 512], in_=xrj[:, n * 512:(n + 1) * 512]),
                  reads=[kx], dsem="st%d_%d" % (j % XRR, n), n=262144)
                yield

        def merged(lists):
            items = []
            for li, atoms in enumerate(lists):
                tot = float(sum(len(a) for a in atoms))
                acc_ = 0
                for k, a in enumerate(atoms):
                    items.append(((acc_ + 0.5 * len(a)) / tot, li, k, a))
                    acc_ += len(a)
            items.sort(key=lambda t: (t[0], t[1], t[2]))
            return [o for t in items for o in t[3]]

        class AtomList:
            def __init__(self):
                self.atoms = []
                self.cur = []

            def add(self, *a, **k):
                self.cur.append((a, k))

            def cut(self):
                if self.cur:
                    self.atoms.append(self.cur)
                    self.cur = []

            def adv(self, g, n=1):
                for _ in range(n):
                    if g is None:
                        return
                    try:
                        next(g)
                    except StopIteration:
                        pass
                    self.cut()

        sb_i = [0]
        H_ENG = "pool"
        for s in range(NT + 3):
            if s + 2 < NT:
                load_x(s + 2)
            if s == NHALO + 3:
                late_setup()
                load_xr(NHALO)
                load_xr(NHALO + 1)
            elif NHALO + 3 < s and s - 2 < NT:
                load_xr(s - 2)
            L1 = AtomList()
            L2 = AtomList()
            g1 = stage1(s, L1.add) if s < NT else None
            g2 = stage2(s - 1, L1.add) if 0 <= s - 1 < NT else None
            g3 = stage3(s - 2, L2.add) if NHALO <= s - 2 < NT else None
            g4 = stage4(s - 3, L2.add) if NHALO <= s - 3 < NT else None
            L1.adv(g1, 1)
            L1.adv(g2, 3)
            L1.adv(g1, 1)
            L1.adv(g2, 8)
            L2.adv(g3, 4)
            L2.adv(g4, 1)
            L2.adv(g3, 2)
            L2.adv(g4, 1)
            L2.adv(g3, 2)
            L2.adv(g4, 1)
            L2.adv(g3, 4)
            lists = [L.atoms for L in (L1, L2) if L.atoms]
            for a, k in merged(lists):
                P.op(*a, **k)
        P.emit(final_wait_eng="pool")
    return nc


def _band_mats0(first):
    W = (2, 4, 8, 16)
    Ac = np.zeros((4, 128, 128), np.float32)
    Ap = np.zeros((4, 128, 128), np.float32)
    ic = np.zeros((128, 4), np.float32)
    for g, w in enumerate(W):
        for t in range(128):
            cnt = min(t + 1, w) if first else w
            ic[t, g] = np.float32(1.0) / np.float32(cnt)
            for d in range(w):
                tp = t - d
                if tp >= 0:
                    Ac[g, t, tp] += 1.0
                elif not first:
                    Ap[g, t, 128 + tp] += 1.0
            Ac[g, t, t] -= cnt
    bf = ml_dtypes.bfloat16
    AcT = np.ascontiguousarray(Ac.transpose(2, 0, 1).reshape(128, 512)).astype(bf)
    ApT = np.ascontiguousarray(Ap.transpose(2, 0, 1).reshape(128, 512)).astype(bf)
    return AcT, ApT, ic


def _band_mats(first):
    W = (2, 4, 8, 16)
    Ac = np.zeros((4, 128, 128), np.float32)
    Ap = np.zeros((4, 128, 128), np.float32)
    for g, w in enumerate(W):
        for t in range(128):
            cnt = min(t + 1, w) if first else w
            for d in range(w):
                tp = t - d
                if tp >= 0:
                    Ac[g, t, tp] += 1.0 / cnt
                elif not first:
                    Ap[g, t, 128 + tp] += 1.0 / cnt
            Ac[g, t, t] -= 1.0
    bf = ml_dtypes.bfloat16
    AcT = np.ascontiguousarray(Ac.transpose(2, 0, 1).reshape(128, 512)).astype(bf)
    ApT = np.ascontiguousarray(Ap.transpose(2, 0, 1).reshape(128, 512)).astype(bf)
    return AcT, ApT


def _bias_layout(rel_bias):
    t = np.arange(5)[:, None, None]
    j = np.arange(128)[None, :, None]
    a = np.arange(128)[None, None, :]
    rel = (t - 4) * 128 + j - a
    idx = np.clip(rel, -256, 256) + 256
    b = rel_bias[:, idx]
    invalid = ((t == 0) & (j < 64) & (a >= 64)) | ((t == 4) & (j >= 64) & (a < 64))
    b = np.where(invalid[None], np.float32(NEG), b).astype(np.float32)
    b = b.reshape(4, 2, 5, 128, 128)
    return np.ascontiguousarray(b.transpose(3, 2, 1, 0, 4).reshape(128, 5 * 2 * 512))


_NC_CACHE = {}


def kernel(x, c, norm_g, w_ada, b_ada, w_in, q_norm_g, k_norm_g, rel_bias, w_pool, pool_scale, w_out):
    x = np.asarray(x, np.float32)
    c = np.asarray(c, np.float32)
    f = lambda a: np.ascontiguousarray(np.asarray(a, np.float32))
    bf = ml_dtypes.bfloat16
    AcT, ApT = _band_mats(False)
    first0 = _band_mats0(True)
    first1 = _band_mats0(False)
    biasT = _bias_layout(f(rel_bias)[0])
    wpl = np.ascontiguousarray(f(w_pool)[0].transpose(1, 0, 2).reshape(128, 512))
    gq2 = np.ascontiguousarray(np.tile(f(q_norm_g)[0], 2).reshape(128, 1))
    gk2 = np.ascontiguousarray(np.tile(f(k_norm_g)[0], 2).reshape(128, 1))
    ident = np.eye(128, dtype=np.float32).astype(bf)
    common = {
        "w_ada": f(w_ada)[0], "b_ada": f(b_ada).reshape(1, 3 * D), "norm_g": f(norm_g).reshape(1, D),
        "w_in": f(w_in)[0], "w_out": f(w_out)[0], "gq2": gq2, "gk2": gk2, "biasT": biasT, "wpl": wpl,
        "pscale": f(pool_scale).reshape(1, 512), "AcT": AcT, "ApT": ApT, "ident": ident,
    }
    in_maps = []
    for r in range(NCORES):
        b, half = r // 2, r % 2
        t0 = half * TOWN
        m = dict(common)
        m["xo"] = np.ascontiguousarray(x[b, t0:t0 + TOWN])
        if half == 0:
            m["xh"] = np.zeros((NHALO * 128, D), np.float32)
            m["hv"] = np.zeros((128, 1), np.float32)
            m["Ac0T"], m["Ap0T"], m["icnt0"] = first0
        else:
            m["xh"] = np.ascontiguousarray(x[b, t0 - NHALO * 128:t0])
            m["hv"] = np.ones((128, 1), np.float32)
            m["Ac0T"], m["Ap0T"], m["icnt0"] = first1
        m["c8"] = np.ascontiguousarray(c[b].reshape(8, 128).T)
        in_maps.append(m)
    if "nc" not in _NC_CACHE:
        _NC_CACHE["nc"] = build_program()
    res = run_bass_kernel_spmd(_NC_CACHE["nc"], in_maps, core_ids=list(range(NCORES)))
    outp = np.empty((4, 2 * TOWN, D), np.float32)
    for r in range(NCORES):
        b, half = r // 2, r % 2
        outp[b, half * TOWN:(half + 1) * TOWN] = np.asarray(res.results[r]["out"], np.float32)
    return outp
```

```python
import numpy as np
import ml_dtypes
from contextlib import ExitStack
import concourse.bass as bass
import concourse.mybir as mybir
from concourse.bass_utils import run_bass_kernel_spmd

F32 = mybir.dt.float32
BF16 = mybir.dt.bfloat16
AF = mybir.ActivationFunctionType
ALU = mybir.AluOpType
AX = mybir.AxisListType

D = 1024
NCORES = 8
TOWN = 4096
NHALO = 4
NOWN = 32
NT = NHALO + NOWN
NSLOT = 8
XR = 3
XRR = 3
NHT = 2
NQT = 3
NUB = 3
NYB = 2
NSZ = 3
NSB = 3
NPT = 4
NSTG = 8
EPS = 1e-6
NEG = -30000.0
SCHED_LAT = 0.45
SCHED_SEED = 0
SCHED_AMP = 0.0


class _Op:
    __slots__ = ("eng", "fn", "reads", "writes", "dsem", "deps", "signal", "sig", "odeps", "busy", "lat", "idx", "t_end", "nb", "gb", "gl")

    def __init__(self, eng, fn, reads, writes, dsem):
        self.odeps = []
        self.busy = 0.0
        self.lat = 0.0
        self.idx = 0
        self.t_end = None
        self.nb = 0
        self.eng = eng
        self.fn = fn
        self.reads = reads
        self.writes = writes
        self.dsem = dsem
        self.deps = []
        self.signal = dsem is not None
        self.sig = None


class Prog:
    ENGS = ("pe", "act", "dve", "pool", "sp")

    def __init__(self, nc):
        self.nc = nc
        self.ops = []
        self.last_writer = {}
        self.readers = {}
        self.dsem_names = []
        self.setup_keys = {}
        self.alias_keys = {}

    @staticmethod
    def _cost(eng, dsem, n):
        if dsem is not None:
            nb = 524288 if n is None else n
            return (0.45 if eng != "pool" else 1.0), 2.0 + nb / 240e3
        if n is None:
            n = 512
        if eng == "pe":
            b = 0.012 + max(n, 48) / 2300.0
        elif eng == "act":
            b = 0.2 + n / 1200.0
        elif eng == "dve":
            b = 0.07 + n / 960.0
        else:
            b = 0.3 + n / 300.0
        return b, b + SCHED_LAT

    def op(self, eng, fn, reads=(), writes=(), dsem=None, n=None):
        reads = list(reads)
        writes = list(writes)
        for g in self.setup_keys:
            if any(k in self.setup_keys[g] for k in reads + writes):
                reads.append("SS_" + g)
            if any(k in self.alias_keys[g] for k in writes):
                reads.append("SSF_" + g)
        o = _Op(eng, fn, reads, writes, dsem)
        o.busy, o.lat = self._cost(eng, dsem, n)
        if dsem is not None:
            o.nb = 524288 if n is None else n
        o.idx = len(self.ops)
        deps = set()
        for b in reads:
            w = self.last_writer.get(b)
            if w is not None:
                deps.add(w)
        for b in writes:
            w = self.last_writer.get(b)
            if w is not None:
                deps.add(w)
            deps.update(self.readers.get(b, ()))
        deps.discard(o)
        for d in deps:
            if d.eng == "pe" and eng == "pe" and d.dsem is None and dsem is None:
                o.odeps.append(d)
                continue
            o.deps.append(d)
        for b in reads:
            self.readers.setdefault(b, []).append(o)
        for b in writes:
            self.last_writer[b] = o
            self.readers[b] = []
        if dsem is not None and dsem not in self.dsem_names:
            self.dsem_names.append(dsem)
        self.ops.append(o)
        return o

    def schedule(self, window=100000):
        pend = {e: [] for e in self.ENGS}
        for o in self.ops:
            pend[o.eng].append(o)
        head = {e: 0 for e in self.ENGS}
        free = {e: 0.0 for e in self.ENGS}
        order = {e: [] for e in self.ENGS}
        remaining = len(self.ops)
        dma_free = 0.0
        rs = np.random.RandomState(SCHED_SEED)
        for o in self.ops:
            f = 1.0 + SCHED_AMP * (2.0 * rs.rand() - 1.0) if SCHED_AMP > 0 else 1.0
            o.gb = o.busy * f
            o.gl = o.lat * f
        while remaining:
            best = None
            for e in self.ENGS:
                L = pend[e]
                h = head[e]
                while h < len(L) and L[h].t_end is not None:
                    h += 1
                head[e] = h
                seen = 0
                k = h
                while k < len(L) and seen < window:
                    o = L[k]
                    k += 1
                    if o.t_end is not None:
                        continue
                    seen += 1
                    ok = True
                    rdy = free[e]
                    for d in o.deps:
                        if d.t_end is None:
                            ok = False
                            break
                        if d.t_end > rdy:
                            rdy = d.t_end
                    if not ok:
                        continue
                    for d in o.odeps:
                        if d.t_end is None:
                            ok = False
                            break
                    if not ok:
                        continue
                    key = (rdy, o.idx)
                    if best is None or key < best[0]:
                        best = (key, o)
                    if rdy <= free[e]:
                        break
            key, o = best
            st_ = key[0]
            free[o.eng] = st_ + o.gb
            if o.dsem is not None:
                xs = max(st_, dma_free)
                dma_free = xs + o.nb / 240e3
                o.gl = dma_free + 2.0 - st_
            o.t_end = st_ + o.gl
            order[o.eng].append(o)
            remaining -= 1
        self.sim_time = max(free.values())
        self.ops = sorted(self.ops, key=lambda o: (o.t_end - o.gl, o.idx))
        return order

    def emit(self, final_wait_eng="pool", reorder=True):
        nc = self.nc
        if reorder:
            self.schedule()
        pos = {}
        for k, o in enumerate(self.ops):
            pos[o] = k
        for o in self.ops:
            keep = []
            last = {}
            for d in o.deps:
                if d.dsem is not None:
                    keep.append(d)
                elif d.eng not in last or pos[d] > pos[last[d.eng]]:
                    last[d.eng] = d
            keep.extend(last.values())
            o.deps = keep
            for d in keep:
                d.signal = True
        cnt = {e: 0 for e in self.ENGS}
        dcnt = {d: 0 for d in self.dsem_names}
        for o in self.ops:
            if o.dsem is not None:
                dcnt[o.dsem] += 16
                o.sig = ("d", o.dsem, dcnt[o.dsem])
            elif o.signal:
                cnt[o.eng] += 1
                o.sig = ("e", o.eng, cnt[o.eng])
        for o in self.ops:
            if o.dsem is not None and o.dsem.startswith("G:"):
                o.sig = ("d", o.dsem, dcnt[o.dsem])
        with ExitStack() as st:
            esem = {e: st.enter_context(nc.semaphore("s_" + e)) for e in self.ENGS}
            dsem = {d: st.enter_context(nc.semaphore("d_%d" % i)) for i, d in enumerate(self.dsem_names)}
            block = st.enter_context(nc.Block())

            def run(engname, eng):
                waited = {}
                for o in self.ops:
                    if o.eng != engname:
                        continue
                    need = {}
                    for d in o.deps:
                        k = d.sig[:2]
                        if d.sig[2] > need.get(k, 0):
                            need[k] = d.sig[2]
                    for k, v in need.items():
                        if waited.get(k, 0) >= v:
                            continue
                        waited[k] = v
                        eng.wait_ge(esem[k[1]] if k[0] == "e" else dsem[k[1]], v)
                    ins = o.fn(eng)
                    if o.dsem is not None:
                        ins.then_inc(dsem[o.dsem], 16)
                    elif o.signal:
                        ins.then_inc(esem[o.eng], 1)
                if engname == final_wait_eng:
                    for d in self.dsem_names:
                        if dcnt[d] > waited.get(("d", d), 0):
                            eng.wait_ge(dsem[d], dcnt[d])

            @block.tensor
            def _(e):
                run("pe", e)

            @block.scalar
            def _(e):
                run("act", e)

            @block.vector
            def _(e):
                run("dve", e)

            @block.gpsimd
            def _(e):
                run("pool", e)

            @block.sync
            def _(e):
                run("sp", e)


def build_program():
    nc = bass.Bass("TRN2", target_bir_lowering=False)

    def din(name, shape, dt=F32):
        return nc.dram_tensor(name, shape, dt, kind="ExternalInput").ap()

    xo = din("xo", [TOWN, D])
    xh = din("xh", [NHALO * 128, D])
    c8_d = din("c8", [128, 8])
    hv_d = din("hv", [128, 1])
    w_ada = din("w_ada", [D, 3 * D])
    b_ada = din("b_ada", [1, 3 * D])
    norm_g = din("norm_g", [1, D])
    w_in = din("w_in", [D, 3 * D])
    w_out = din("w_out", [D, D])
    gq2_d = din("gq2", [128, 1])
    gk2_d = din("gk2", [128, 1])
    biasT_d = din("biasT", [128, 5 * 2 * 512])
    wpl_d = din("wpl", [128, 512])
    pscale = din("pscale", [1, 512])
    Ac_d = din("AcT", [128, 512], BF16)
    Ap_d = din("ApT", [128, 512], BF16)
    Ac0_d = din("Ac0T", [128, 512], BF16)
    Ap0_d = din("Ap0T", [128, 512], BF16)
    ident_d = din("ident", [128, 128], BF16)
    icnt0_d = din("icnt0", [128, 4])
    out = nc.dram_tensor("out", [TOWN, D], F32, kind="ExternalOutput").ap()

    with ExitStack() as st:
        def sb(name, shape, dt):
            return st.enter_context(nc.sbuf_tensor("sb_" + name, shape, dt))

        Wb = sb("Wb", [128, 8, 3 * D], BF16)
        Wob = sb("Wob", [128, 8, D], BF16)
        wpb = sb("wpb", [128, 4, 128], BF16)
        biasT = sb("biasT", [128, 5, 2, 512], F32)
        Ac = sb("Ac", [128, 512], BF16)
        Ap = sb("Ap", [128, 512], BF16)
        Ac0 = sb("Ac0", [128, 512], BF16)
        Ap0 = sb("Ap0", [128, 512], BF16)
        ident = sb("ident", [128, 128], BF16)
        mod = sb("mod", [128, 3 * D], F32)
        c8 = sb("c8", [128, 8], F32)
        hv = sb("hv", [128, 1], F32)
        gq2 = sb("gq2", [128, 1], F32)
        gk2 = sb("gk2", [128, 1], F32)
        G = sb("G", [128, 1], F32)
        icnt0 = sb("icnt0", [128, 4], F32)
        ones = sb("ones", [128, 128], F32)
        fence = sb("fence", [128, 2], F32)
        kT = sb("kT", [128, NSLOT, 4, 128], BF16)
        Vaug = sb("Vaug", [128, NSLOT, 8, 66], BF16)
        xt = [sb("xt%d" % i, [128, D], F32) for i in range(XR)]
        hb = [sb("hb%d" % i, [128, D], BF16) for i in range(2)]
        hT = [sb("hT%d" % i, [128, 8, 128], BF16) for i in range(NHT)]
        sqf = [sb("sqf%d" % i, [128, 512], F32) for i in range(2)]
        qkn = [sb("qkn%d" % i, [128, 512], BF16) for i in range(4)]
        qT = [sb("qT%d" % i, [128, 4, 128], BF16) for i in range(NQT)]
        ub = [sb("ub%d" % i, [128, 512], BF16) for i in range(NUB)]
        ybt = [sb("yb%d" % i, [128, D], BF16) for i in range(NYB)]
        yTt = [sb("yT%d" % i, [128, D], BF16) for i in range(2)]
        ss = sb("ss", [128, 2], F32)
        lnv = sb("lnv", [128, 2], F32)
        rstd = sb("rstd", [128, 2], F32)
        ssq = [sb("ssq%d" % i, [128, 8], F32) for i in range(2)]
        lnq = [sb("lnq%d" % i, [128, 8], F32) for i in range(2)]
        rq = [sb("rq%d" % i, [128, 8], F32) for i in range(2)]
        rden = [sb("rden%d" % i, [128, 2, 4], F32) for i in range(2)]
        shE = sb("shE", [128, 20 * 256], F32)
        shL = sb("shL", [128, NSTG * 512], F32)

        def she(a_kb, b_kb):
            return shE[:, int(a_kb * 256):int(b_kb * 256)]

        acc = she(0, 12)
        ng_bc = she(12, 16)
        ps_bc = she(16, 18)
        wst = she(18, 20)
        sz = [she(4 * i, 4 * i + 4) for i in range(NSZ)]
        ez = [she(12, 14), she(14, 16)]
        mixT = [she(16, 17).bitcast(BF16), she(17, 18).bitcast(BF16)]
        on = [she(18, 19), she(19, 20)]
        stage = [shL[:, i * 512:(i + 1) * 512] for i in range(NSTG)]
        xr = [shL[:, i * 1024:(i + 1) * 1024] for i in range(XRR)]
        sbbt = [sb("sbb%d" % i, [128, 512], F32) for i in range(NSB)]
        pTt = [sb("pT%d" % i, [128, 512], BF16) for i in range(NPT)]
        sbb = [t[:] for t in sbbt]
        pT = [t[:] for t in pTt]
        yb = [t[:] for t in ybt]
        yT = [t[:] for t in yTt]

        banks = [st.enter_context(nc.psum_tensor("bank%d" % i, [128, 512], F32)) for i in range(8)]
        banks_bf = [b.bitcast(BF16) for b in banks]

        P = Prog(nc)
        P.setup_keys = {"E": set(["acc", "ng_bc", "ps_bc", "wst"] + [("acc", n) for n in range(4)]),
                        "L": set("stage%d" % i for i in range(NSTG))}
        P.alias_keys = {"E": set([("sz", i) for i in range(4)] + [("ez", i) for i in range(2)] + [("mixT", i) for i in range(2)] + [("on", i) for i in range(2)]),
                        "L": set(("xr", i, h) for i in range(XRR) for h in range(2))}

        def ld(dst, src, key, sem="G:c"):
            P.op("act", lambda e: e.dma_start(out=dst, in_=src), writes=[key], dsem=sem, n=65536)

        ld(c8[:], c8_d, "c8")
        ld(mod[:, 0:2 * D], b_ada[:, 0:2 * D].partition_broadcast(128), "modb")
        ld(ng_bc, norm_g.partition_broadcast(128), "ng_bc")
        ld(ident[:], ident_d, "ident")
        ld(hv[:], hv_d, "hv")
        ld(icnt0[:], icnt0_d, "icnt0")
        ld(gq2[:], gq2_d, "gq2")
        ld(gk2[:], gk2_d, "gk2")
        ld(Ac[:], Ac_d, "Ac")
        ld(Ap[:], Ap_d, "Ap")
        ld(Ac0[:], Ac0_d, "Ac0")
        ld(Ap0[:], Ap0_d, "Ap0")
        ld(wst, wpl_d, "wst")
        ld(ps_bc, pscale.partition_broadcast(128), "ps_bc")

        P.op("dve", lambda e: e.memset(ones[:], 1.0), writes=["ones"], n=128)
        P.op("pool", lambda e: e.memset(Vaug[:].rearrange("p a b c -> p (a b c)"), 1.0),
             writes=[("V", s) for s in range(NSLOT)] + [("V1", s) for s in range(NSLOT)], n=1000)
        P.op("dve", lambda e: e.scalar_tensor_tensor(out=G[:], in0=gq2[:], scalar=0.125, in1=gk2[:], op0=ALU.mult, op1=ALU.mult),
             reads=["gq2", "gk2"], writes=["G"], n=1)
        P.op("dve", lambda e: e.tensor_tensor(out=wpb[:].rearrange("p g o -> p (g o)"), in0=wst, in1=ps_bc, op=ALU.mult),
             reads=["wst", "ps_bc"], writes=["wpb"])

        stg_i = [0]

        def staged(src, consumer):
            s = stg_i[0] % NSTG
            stg_i[0] += 1
            P.op("sp", lambda e: e.dma_start(out=stage[s], in_=src), writes=["stage%d" % s], dsem="stg%d" % s, n=262144)
            consumer(stage[s], "stage%d" % s)

        arot = [0]
        brot = [0]

        def bankA():
            i = 2 + arot[0] % 3
            arot[0] += 1
            return i

        def bankB():
            i = 5 + brot[0] % 3
            brot[0] += 1
            return i

        def load_x(j):
            src = xh[j * 128:(j + 1) * 128, :] if j < NHALO else xo[(j - NHALO) * 128:(j - NHALO + 1) * 128, :]
            P.op("sp", lambda e: e.dma_start(out=xt[j % XR][:], in_=src), writes=[("xt", j % XR)], dsem="ld%d" % (j % XR))

        def ada_block(n, fin=True):
            cs = slice(n * 512, (n + 1) * 512)
            late = n >= 4
            accn = mod[:, cs] if late else acc[:, cs]
            kacc = ("accg", n) if late else ("acc", n)
            for kc in range(8):
                def cons(stg, key, kc=kc):
                    if kc == 0:
                        P.op("dve", lambda e: e.tensor_scalar(out=accn, in0=stg, scalar1=c8[:, 0:1], scalar2=None, op0=ALU.mult),
                             reads=[key, "c8"], writes=[kacc])
                    else:
                        P.op("dve", lambda e: e.scalar_tensor_tensor(out=accn, in0=stg, scalar=c8[:, kc:kc + 1],
                                                                     in1=accn, op0=ALU.mult, op1=ALU.add),
                             reads=[key, "c8", kacc], writes=[kacc])
                staged(w_ada[kc * 128:(kc + 1) * 128, cs], cons)
            def fin_part():
                bi = bankA() if not late else bankB()
                P.op("pe", lambda e: e.matmul(banks[bi][:, :], lhsT=ones[:], rhs=accn, start=True, stop=True),
                     reads=["ones", kacc], writes=[("bank", bi)], n=2048)
                if late:
                    def cons2(stg, key):
                        P.op("dve", lambda e: e.tensor_tensor(out=mod[:, cs], in0=banks[bi][:, :], in1=stg, op=ALU.add),
                             reads=[("bank", bi), key, kacc], writes=[("mod", n), kacc])
                    staged(b_ada[:, cs].partition_broadcast(128), cons2)
                else:
                    P.op("dve", lambda e: e.tensor_tensor(out=mod[:, cs], in0=banks[bi][:, :], in1=mod[:, cs], op=ALU.add),
                         reads=[("bank", bi), "modb"], writes=[("mod", n)])
            if fin:
                fin_part()
                return None
            return fin_part

        for n in range(4):
            ada_block(n)
        P.op("dve", lambda e: e.scalar_tensor_tensor(out=mod[:, D:2 * D], in0=mod[:, D:2 * D], scalar=1.0, in1=ng_bc, op0=ALU.add, op1=ALU.mult),
             reads=[("mod", 2), ("mod", 3), "ng_bc"], writes=[("mod", 2), ("mod", 3)], n=1024)
        P.op("dve", lambda e: e.memset(fence[:, 0:1], 0.0), writes=["SS_E", "SSF_E"], n=1)
        shift_row = mod[:, 0:D]
        gs_row = mod[:, D:2 * D]
        gate_row = mod[:, 2 * D:3 * D]
        load_x(0)
        load_x(1)
        ci = [0]
        for n in (1, 2, 3, 0, 4, 5):
            cs = slice(n * 512, (n + 1) * 512)
            for kc in range(8):
                def cons(stg, key, kc=kc, n=n, cs=cs):
                    ci[0] += 1
                    if ci[0] % 2 == 0:
                        P.op("dve", lambda e: e.tensor_copy(out=Wb[:, kc, cs], in_=stg), reads=[key], writes=[("Wb", kc, n)])
                    else:
                        P.op("act", lambda e: e.activation(out=Wb[:, kc, cs], in_=stg, func=AF.Copy), reads=[key], writes=[("Wb", kc, n)])
                staged(w_in[kc * 128:(kc + 1) * 128, cs], cons)
        for g in range(4):
            c0 = 1536 + g * 128
            scr = yT[g % 2]
            kscr = ("yT", g % 2)
            bt = bankB()
            for kc in range(8):
                P.op("pe", lambda e, kc=kc, bt=bt, c0=c0: e.transpose(out=banks_bf[bt][:, kc * 128:(kc + 1) * 128], in_=Wb[:, kc, c0:c0 + 128], identity=ident[:]),
                     reads=[("Wb", kc, 3), "ident"], writes=[("bank", bt)], n=128)
            P.op("dve", lambda e, bt=bt, scr=scr: e.tensor_copy(out=scr, in_=banks_bf[bt][:, :]), reads=[("bank", bt)], writes=[kscr], n=600)
            for half in range(2):
                bm = bankB()
                for q in range(4):
                    kc = half * 4 + q
                    P.op("pe", lambda e, kc=kc, q=q, bm=bm, scr=scr, g=g: e.matmul(banks[bm][:, q * 128:(q + 1) * 128], lhsT=scr[:, kc * 128:(kc + 1) * 128],
                                                                                  rhs=wpb[:, g, :], start=True, stop=True),
                         reads=[kscr, "wpb"], writes=[("bank", bm)], n=128)
                kws = [("Wb", kc, 3) for kc in range(half * 4, half * 4 + 4)]
                P.op("act", lambda e, half=half, bm=bm, c0=c0: e.activation(out=Wb[:, half * 4:half * 4 + 4, c0:c0 + 128],
                                                                            in_=banks[bm][:, :].rearrange("p (a b) -> p a b", b=128), func=AF.Copy),
                     reads=[("bank", bm)] + kws, writes=kws)
        P.op("sp", lambda e: e.dma_start(out=biasT[:].rearrange("p a b c -> p (a b c)"), in_=biasT_d), reads=[("Wb", 7, 5)], writes=["biasT"], dsem="bias", n=2621440)
        late_fins = [ada_block(n, fin=False) for n in (4, 5)]

        def late_setup():
            for f in late_fins:
                f()
            bankB()
            for n in range(2):
                cs = slice(n * 512, (n + 1) * 512)
                for kc in range(8):
                    def cons(stg, key, kc=kc, n=n, cs=cs):
                        if kc % 2 == 0:
                            P.op("pool", lambda e: e.tensor_tensor(out=Wob[:, kc, cs], in0=stg, in1=gate_row[:, cs], op=ALU.mult),
                                 reads=[key, ("mod", 4 + n)], writes=[("Wob", kc, n)], n=330)
                        else:
                            P.op("dve", lambda e: e.tensor_tensor(out=Wob[:, kc, cs], in0=stg, in1=gate_row[:, cs], op=ALU.mult),
                                 reads=[key, ("mod", 4 + n)], writes=[("Wob", kc, n)])
                    staged(w_out[kc * 128:(kc + 1) * 128, cs], cons)
            P.op("dve", lambda e: e.memset(fence[:, 1:2], 0.0), writes=["SS_L", "SSF_L"], n=1)

        def load_xr(j):
            i = j - NHALO
            P.op("sp", lambda e: e.dma_start(out=xr[j % XRR], in_=xo[i * 128:(i + 1) * 128, :]), writes=[("xr", j % XRR, 0), ("xr", j % XRR, 1)], dsem="lr%d" % (j % XRR))

        def stage1(j, A):
            xtj = xt[j % XR]
            hj = hb[j % 2]
            hTj = hT[j % NHT]
            kx = ("xt", j % XR)
            kh = ("hb", j % 2)
            khT = ("hT", j % NHT)
            A("act", lambda e: e.activation(out=hj[:], in_=xtj[:], func=AF.Square, accum_out=ss[:, 0:1]), reads=[kx], writes=[kh, "ss"], n=1024)
            A("act", lambda e: e.activation(out=lnv[:, 0:1], in_=ss[:, 0:1], func=AF.Ln, scale=1.0 / D, bias=EPS), reads=["ss"], writes=["lnv"], n=1)
            A("act", lambda e: e.activation(out=rstd[:, 0:1], in_=lnv[:, 0:1], func=AF.Exp, scale=-0.5), reads=["lnv"], writes=["rstd"], n=1)
            A("dve", lambda e: e.scalar_tensor_tensor(out=xtj[:], in0=xtj[:], scalar=rstd[:, 0:1], in1=gs_row, op0=ALU.mult, op1=ALU.mult),
              reads=[kx, "rstd", ("mod", 2), ("mod", 3)], writes=[kx], n=1024)
            A(H_ENG, lambda e: e.tensor_tensor(out=hj[:], in0=xtj[:], in1=shift_row, op=ALU.add),
              reads=[kx, ("mod", 0), ("mod", 1)], writes=[kh], n=1024)
            yield
            bi = bankA()
            for kc in range(8):
                A("pe", lambda e, kc=kc, bi=bi: e.transpose(out=banks_bf[bi][:, kc * 128:(kc + 1) * 128], in_=hj[:, kc * 128:(kc + 1) * 128], identity=ident[:]),
                  reads=[kh, "ident"], writes=[("bank", bi)], n=128)
            A("dve", lambda e, bi=bi: e.tensor_copy(out=hTj[:].rearrange("p a b -> p (a b)"), in_=banks_bf[bi][:, :]),
              reads=[("bank", bi)], writes=[khT], n=600)
            yield

        def stage2(j, A):
            halo = j < NHALO
            slot = j % NSLOT
            hTj = hT[j % NHT]
            khT = ("hT", j % NHT)
            chunks = ([2] + ([3] if j == NHALO - 1 else []) + [1]) if halo else [2, 3, 0, 1, 4, 5]
            deferred = []
            for n in chunks:
                bi = bankA()
                kb = ("bank", bi)
                for kc in range(8):
                    A("pe", lambda e, kc=kc, bi=bi, n=n: e.matmul(banks[bi][:, :], lhsT=hTj[:, kc, :], rhs=Wb[:, kc, n * 512:(n + 1) * 512],
                                                                   start=(kc == 0), stop=(kc == 7)),
                      reads=[khT, ("Wb", kc, n)], writes=[kb])
                if n in (0, 1):
                    r = n
                    qn = qkn[(j % 2) * 2 + n]
                    kqn = ("qkn", (j % 2) * 2 + n)
                    A("act", lambda e, bi=bi, r=r: e.activation(out=sqf[r][:], in_=banks[bi][:, :], func=AF.Square), reads=[kb], writes=[("sqf", r)])
                    A("dve", lambda e, r=r: e.tensor_reduce(out=ssq[r][:], in_=sqf[r][:].rearrange("p (h d) -> p h d", d=64), axis=AX.X, op=ALU.add),
                      reads=[("sqf", r)], writes=[("ssq", r)])
                    A("act", lambda e, r=r: e.activation(out=lnq[r][:], in_=ssq[r][:], func=AF.Ln, scale=1.0 / 64, bias=EPS), reads=[("ssq", r)], writes=[("lnq", r)], n=8)
                    A("act", lambda e, r=r: e.activation(out=rq[r][:], in_=lnq[r][:], func=AF.Exp, scale=-0.5), reads=[("lnq", r)], writes=[("rq", r)], n=8)
                    A("dve", lambda e, bi=bi, r=r, qn=qn: e.tensor_tensor(out=qn[:].rearrange("p (h d) -> p h d", d=64),
                                                                         in0=banks[bi][:, :].rearrange("p (h d) -> p h d", d=64),
                                                                         in1=rq[r][:].unsqueeze(2).to_broadcast([128, 8, 64]), op=ALU.mult),
                      reads=[kb, ("rq", r)], writes=[kqn])
                    deferred.append((n, qn, kqn))
                elif n == 2:
                    vdst = Vaug[:, slot, :, 0:64]
                    vsrc = banks[bi][:, :].rearrange("p (h d) -> p h d", d=64)
                    if halo:
                        A("dve", lambda e, vdst=vdst, vsrc=vsrc: e.tensor_scalar(out=vdst, in0=vsrc, scalar1=hv[:, 0:1], scalar2=None, op0=ALU.mult),
                          reads=[kb, "hv"], writes=[("V", slot)])
                        A("dve", lambda e: e.tensor_copy(out=Vaug[:, slot, :, 64:65], in_=hv[:, 0:1].unsqueeze(1).to_broadcast([128, 8, 1])),
                          reads=["hv"], writes=[("V1", slot)], n=8)
                    else:
                        A("act", lambda e, vdst=vdst, vsrc=vsrc: e.activation(out=vdst, in_=vsrc, func=AF.Copy), reads=[kb], writes=[("V", slot)])
                        if NSLOT <= j < NSLOT + NHALO:
                            A("dve", lambda e: e.memset(Vaug[:, slot, :, 64:65], 1.0), writes=[("V1", slot)], n=8)
                elif n == 3:
                    A("act", lambda e, bi=bi: e.activation(out=ub[j % NUB][:], in_=banks[bi][:, :], func=AF.Copy), reads=[kb], writes=[("ub", j % NUB)])
                else:
                    zi = n - 4
                    ezz = ez[zi]
                    kez = ("ez", zi)
                    A("act", lambda e, bi=bi, ezz=ezz: e.activation(out=ezz, in_=banks[bi][:, :], func=AF.Exp, scale=-1.0), reads=[kb], writes=[kez])
                    A("act", lambda e, ezz=ezz: e.activation(out=ezz, in_=ezz, func=AF.Ln, bias=1.0), reads=[kez], writes=[kez])
                    A("act", lambda e, ezz=ezz: e.activation(out=ezz, in_=ezz, func=AF.Exp, scale=-1.0), reads=[kez], writes=[kez])
                    A("dve", lambda e, bi=bi, ezz=ezz, zi=zi: e.tensor_tensor(out=sz[j % NSZ][:, zi * 512:(zi + 1) * 512], in0=banks[bi][:, :], in1=ezz, op=ALU.mult),
                      reads=[kb, kez], writes=[("sz", j % NSZ)])
                yield
            for n, qn, kqn in deferred:
                bt = bankA()
                for hp in range(4):
                    A("pe", lambda e, hp=hp, bt=bt, qn=qn: e.transpose(out=banks_bf[bt][:, hp * 128:(hp + 1) * 128], in_=qn[:, hp * 128:(hp + 1) * 128], identity=ident[:]),
                      reads=[kqn, "ident"], writes=[("bank", bt)], n=128)
                if n == 0:
                    A("dve", lambda e, bt=bt: e.tensor_scalar(out=qT[j % NQT][:].rearrange("p a b -> p (a b)"), in0=banks_bf[bt][:, 0:512],
                                                             scalar1=G[:, 0:1], scalar2=None, op0=ALU.mult),
                      reads=[("bank", bt), "G"], writes=[("qT", j % NQT)])
                else:
                    A("act", lambda e, bt=bt: e.activation(out=kT[:, slot].rearrange("p a b -> p (a b)"), in_=banks_bf[bt][:, 0:512], func=AF.Copy),
                      reads=[("bank", bt)], writes=[("kT", slot)])
                yield

        def stage3(j, B):
            i = j - NHALO
            szj = sz[j % NSZ]
            ksz = ("sz", j % NSZ)
            y = yb[j % NYB]
            ky = ("yb", j % NYB)
            Acur = Ac0 if i == 0 else Ac
            Aprv = Ap0 if i == 0 else Ap
            kAc = "Ac0" if i == 0 else "Ac"
            kAp = "Ap0" if i == 0 else "Ap"
            ucur = ub[j % NUB]
            uprv = ub[(j - 1) % NUB]
            mx = mixT[i % 2]

            def pool_m():
                by = bankB()
                bankB()
                for g in range(4):
                    gs_ = slice(g * 128, (g + 1) * 128)
                    B("pe", lambda e, gs_=gs_, by=by: e.matmul(banks[by][:, gs_], lhsT=Acur[:, gs_], rhs=ucur[:, gs_], start=True, stop=False),
                      reads=[("ub", j % NUB), kAc], writes=[("bank", by)], n=128)
                    B("pe", lambda e, gs_=gs_, by=by: e.matmul(banks[by][:, gs_], lhsT=Aprv[:, gs_], rhs=uprv[:, gs_], start=False, stop=True),
                      reads=[("ub", (j - 1) % NUB), kAp], writes=[("bank", by)], n=128)
                if i == 0:
                    B("dve", lambda e, by=by: e.tensor_tensor(out=banks[by][:, :].rearrange("p (g c) -> p g c", c=128),
                                                              in0=banks[by][:, :].rearrange("p (g c) -> p g c", c=128),
                                                              in1=icnt0[:].unsqueeze(2).to_broadcast([128, 4, 128]), op=ALU.mult),
                      reads=[("bank", by), "icnt0"], writes=[("bank", by)])
                B("dve", lambda e, by=by: e.tensor_tensor(out=y[:, 512:1024], in0=banks[by][:, :], in1=szj[:, 512:1024], op=ALU.mult),
                  reads=[("bank", by), ksz], writes=[ky])

            qTj = qT[j % NQT]
            pts_t = {}

            def qk(t):
                slot = (j - 4 + t) % NSLOT
                bs = [bankB(), bankB()]
                for hh in range(4):
                    for par in range(2):
                        B("pe", lambda e, hh=hh, par=par, bs=bs, slot=slot: e.matmul(
                            banks[bs[par]][:, hh * 128:(hh + 1) * 128],
                            lhsT=kT[par * 64:(par + 1) * 64, slot, hh, :],
                            rhs=qTj[par * 64:(par + 1) * 64, hh, :], start=True, stop=True),
                          reads=[("kT", slot), ("qT", j % NQT)], writes=[("bank", bs[par])], n=64)
                pts = []
                for par in range(2):
                    r = sb_i[0] % NSB
                    pr = sb_i[0] % NPT
                    sb_i[0] += 1
                    B("dve", lambda e, par=par, r=r, bs=bs, t=t: e.tensor_tensor(out=sbb[r], in0=banks[bs[par]][:, :], in1=biasT[:, t, par, :], op=ALU.add),
                      reads=[("bank", bs[par]), "biasT"], writes=[("sbb", r)])
                    B("act", lambda e, r=r, pr=pr: e.activation(out=pT[pr], in_=sbb[r], func=AF.Exp), reads=[("sbb", r)], writes=[("pT", pr)])
                    pts.append(pr)
                pts_t[t] = pts

            def pv(t):
                slot = (j - 4 + t) % NSLOT
                for par in range(2):
                    pr = pts_t[t][par]
                    for hh in range(4):
                        h = 2 * hh + par
                        B("pe", lambda e, par=par, hh=hh, h=h, pr=pr, slot=slot, t=t: e.matmul(
                            banks[par][:, hh * 128:hh * 128 + 65], lhsT=pT[pr][:, hh * 128:(hh + 1) * 128],
                            rhs=Vaug[:, slot, h, 0:65], start=(t == 0 and hh == 0), stop=(t == 4), skip_group_check=True),
                          reads=[("pT", pr), ("V", slot), ("V1", slot)], writes=[("bank", par)], n=65)

            for f, arg in ((qk, 0), (pool_m, None), (qk, 1), (pv, 0), (qk, 2), (pv, 1), (qk, 3), (pv, 2), (qk, 4), (pv, 3), (pv, 4)):
                if arg is None:
                    f()
                else:
                    f(arg)
                yield
            rd = rden[i % 2]
            for par in range(2):
                ov = banks[par][:, :].rearrange("p (h e) -> p h e", e=128)
                B("dve", lambda e, par=par, ov=ov: e.reciprocal(out=rd[:, par, :].unsqueeze(2), in_=ov[:, :, 64:65]),
                  reads=[("bank", par)], writes=[("rden", i % 2, par)], n=4)
                B("dve", lambda e, par=par, ov=ov: e.tensor_tensor(out=on[par].rearrange("p (h d) -> p h d", d=64), in0=ov[:, :, 0:64],
                                                                 in1=rd[:, par, :].unsqueeze(2).to_broadcast([128, 4, 64]), op=ALU.mult),
                  reads=[("bank", par), ("rden", i % 2, par)], writes=[("on", par)], n=256)
                yv = y[:, 0:512].rearrange("p (hh par d) -> p hh par d", par=2, d=64)[:, :, par, :]
                szv = szj[:, 0:512].rearrange("p (hh par d) -> p hh par d", par=2, d=64)[:, :, par, :]
                B("dve", lambda e, par=par, yv=yv, szv=szv: e.tensor_tensor(out=yv, in0=on[par].rearrange("p (h d) -> p h d", d=64), in1=szv, op=ALU.mult),
                  reads=[("on", par), ksz], writes=[ky], n=256)
            yield

        def stage4(j, B):
            i = j - NHALO
            y = yb[j % NYB]
            ky = ("yb", j % NYB)
            xrj = xr[j % XRR]
            kx = ("xr", j % XRR)
            bt = bankB()
            for kc in range(8):
                B("pe", lambda e, kc=kc, bt=bt: e.transpose(out=banks_bf[bt][:, kc * 128:(kc + 1) * 128], in_=y[:, kc * 128:(kc + 1) * 128], identity=ident[:]),
                  reads=[ky, "ident"], writes=[("bank", bt)], n=128)
            yTi = yT[i % 2]
            B("dve", lambda e, bt=bt: e.tensor_copy(out=yTi, in_=banks_bf[bt][:, :]), reads=[("bank", bt)], writes=[("yT", i % 2)], n=600)
            yield
            for n in range(2):
                br = bankB()
                for kc in range(8):
                    B("pe", lambda e, kc=kc, br=br, n=n: e.matmul(banks[br][:, :], lhsT=yTi[:, kc * 128:(kc + 1) * 128], rhs=Wob[:, kc, n * 512:(n + 1) * 512],
                                                                   start=(kc == 0), stop=(kc == 7)),
                      reads=[("yT", i % 2), ("Wob", kc, n)], writes=[("bank", br)])
                B("dve", lambda e, br=br, n=n: e.tensor_tensor(out=xrj[:, n * 512:(n + 1) * 512], in0=banks[br][:, :], in1=xrj[:, n * 512:(n + 1) * 512], op=ALU.add),
                  reads=[("bank", br), (kx[0], kx[1], n)], writes=[(kx[0], kx[1], n)])
                B("sp", lambda e, n=n: e.dma_start(out=out[i * 128:(i + 1) * 128, n * 512:(n + 1) * 512], in_=xrj[:, n * 512:(n + 1) * 512]),
                  reads=[(kx[0], kx[1], n)], dsem="st%d_%d" % (j % XRR, n), n=262144)
                yield

        def merged(lists):
            items = []
            for li, atoms in enumerate(lists):
                tot = float(sum(len(a) for a in atoms))
                acc_ = 0
                for k, a in enumerate(atoms):
                    items.append(((acc_ + 0.5 * len(a)) / tot, li, k, a))
                    acc_ += len(a)
            items.sort(key=lambda t: (t[0], t[1], t[2]))
            return [o for t in items for o in t[3]]

        class AtomList:
            def __init__(self):
                self.atoms = []
                self.cur = []

            def add(self, *a, **k):
                self.cur.append((a, k))

            def cut(self):
                if self.cur:
                    self.atoms.append(self.cur)
                    self.cur = []

            def adv(self, g, n=1):
                for _ in range(n):
                    if g is None:
                        return
                    try:
                        next(g)
                    except StopIteration:
                        pass
                    self.cut()

        sb_i = [0]
        H_ENG = "pool"
        for s in range(NT + 3):
            if s + 2 < NT:
                load_x(s + 2)
            if s == NHALO + 3:
                late_setup()
                load_xr(NHALO)
                load_xr(NHALO + 1)
            elif NHALO + 3 < s and s - 2 < NT:
                load_xr(s - 2)
            L1 = AtomList()
            L2 = AtomList()
            g1 = stage1(s, L1.add) if s < NT else None
            g2 = stage2(s - 1, L1.add) if 0 <= s - 1 < NT else None
            g3 = stage3(s - 2, L2.add) if NHALO <= s - 2 < NT else None
            g4 = stage4(s - 3, L2.add) if NHALO <= s - 3 < NT else None
            L1.adv(g1, 1)
            L1.adv(g2, 3)
            L1.adv(g1, 1)
            L1.adv(g2, 8)
            L2.adv(g3, 4)
            L2.adv(g4, 1)
            L2.adv(g3, 2)
            L2.adv(g4, 1)
            L2.adv(g3, 2)
            L2.adv(g4, 1)
            L2.adv(g3, 4)
            lists = [L.atoms for L in (L1, L2) if L.atoms]
            for a, k in merged(lists):
                P.op(*a, **k)
        P.emit(final_wait_eng="pool")
    return nc


def _band_mats0(first):
    W = (2, 4, 8, 16)
    Ac = np.zeros((4, 128, 128), np.float32)
    Ap = np.zeros((4, 128, 128), np.float32)
    ic = np.zeros((128, 4), np.float32)
    for g, w in enumerate(W):
        for t in range(128):
            cnt = min(t + 1, w) if first else w
            ic[t, g] = np.float32(1.0) / np.float32(cnt)
            for d in range(w):
                tp = t - d
                if tp >= 0:
                    Ac[g, t, tp] += 1.0
                elif not first:
                    Ap[g, t, 128 + tp] += 1.0
            Ac[g, t, t] -= cnt
    bf = ml_dtypes.bfloat16
    AcT = np.ascontiguousarray(Ac.transpose(2, 0, 1).reshape(128, 512)).astype(bf)
    ApT = np.ascontiguousarray(Ap.transpose(2, 0, 1).reshape(128, 512)).astype(bf)
    return AcT, ApT, ic


def _band_mats(first):
    W = (2, 4, 8, 16)
    Ac = np.zeros((4, 128, 128), np.float32)
    Ap = np.zeros((4, 128, 128), np.float32)
    for g, w in enumerate(W):
        for t in range(128):
            cnt = min(t + 1, w) if first else w
            for d in range(w):
                tp = t - d
                if tp >= 0:
                    Ac[g, t, tp] += 1.0 / cnt
                elif not first:
                    Ap[g, t, 128 + tp] += 1.0 / cnt
            Ac[g, t, t] -= 1.0
    bf = ml_dtypes.bfloat16
    AcT = np.ascontiguousarray(Ac.transpose(2, 0, 1).reshape(128, 512)).astype(bf)
    ApT = np.ascontiguousarray(Ap.transpose(2, 0, 1).reshape(128, 512)).astype(bf)
    return AcT, ApT


def _bias_layout(rel_bias):
    t = np.arange(5)[:, None, None]
    j = np.arange(128)[None, :, None]
    a = np.arange(128)[None, None, :]
    rel = (t - 4) * 128 + j - a
    idx = np.clip(rel, -256, 256) + 256
    b = rel_bias[:, idx]
    invalid = ((t == 0) & (j < 64) & (a >= 64)) | ((t == 4) & (j >= 64) & (a < 64))
    b = np.where(invalid[None], np.float32(NEG), b).astype(np.float32)
    b = b.reshape(4, 2, 5, 128, 128)
    return np.ascontiguousarray(b.transpose(3, 2, 1, 0, 4).reshape(128, 5 * 2 * 512))


_NC_CACHE = {}


def kernel(x, c, norm_g, w_ada, b_ada, w_in, q_norm_g, k_norm_g, rel_bias, w_pool, pool_scale, w_out):
    x = np.asarray(x, np.float32)
    c = np.asarray(c, np.float32)
    f = lambda a: np.ascontiguousarray(np.asarray(a, np.float32))
    bf = ml_dtypes.bfloat16
    AcT, ApT = _band_mats(False)
    first0 = _band_mats0(True)
    first1 = _band_mats0(False)
    biasT = _bias_layout(f(rel_bias)[0])
    wpl = np.ascontiguousarray(f(w_pool)[0].transpose(1, 0, 2).reshape(128, 512))
    gq2 = np.ascontiguousarray(np.tile(f(q_norm_g)[0], 2).reshape(128, 1))
    gk2 = np.ascontiguousarray(np.tile(f(k_norm_g)[0], 2).reshape(128, 1))
    ident = np.eye(128, dtype=np.float32).astype(bf)
    common = {
        "w_ada": f(w_ada)[0], "b_ada": f(b_ada).reshape(1, 3 * D), "norm_g": f(norm_g).reshape(1, D),
        "w_in": f(w_in)[0], "w_out": f(w_out)[0], "gq2": gq2, "gk2": gk2, "biasT": biasT, "wpl": wpl,
        "pscale": f(pool_scale).reshape(1, 512), "AcT": AcT, "ApT": ApT, "ident": ident,
    }
    in_maps = []
    for r in range(NCORES):
        b, half = r // 2, r % 2
        t0 = half * TOWN
        m = dict(common)
        m["xo"] = np.ascontiguousarray(x[b, t0:t0 + TOWN])
        if half == 0:
            m["xh"] = np.zeros((NHALO * 128, D), np.float32)
            m["hv"] = np.zeros((128, 1), np.float32)
            m["Ac0T"], m["Ap0T"], m["icnt0"] = first0
        else:
            m["xh"] = np.ascontiguousarray(x[b, t0 - NHALO * 128:t0])
            m["hv"] = np.ones((128, 1), np.float32)
            m["Ac0T"], m["Ap0T"], m["icnt0"] = first1
        m["c8"] = np.ascontiguousarray(c[b].reshape(8, 128).T)
        in_maps.append(m)
    if "nc" not in _NC_CACHE:
        _NC_CACHE["nc"] = build_program()
    res = run_bass_kernel_spmd(_NC_CACHE["nc"], in_maps, core_ids=list(range(NCORES)))
    outp = np.empty((4, 2 * TOWN, D), np.float32)
    for r in range(NCORES):
        b, half = r // 2, r % 2
        outp[b, half * TOWN:(half + 1) * TOWN] = np.asarray(res.results[r]["out"], np.float32)
    return outp
```
